# Optimizing a Trainium2 kernel written in Bass

```python
import jax, jax.numpy as jnp
from jax import lax
import numpy as np

D_MODEL = 1024
BATCH = 8
SEQ = 4096
DEPTH = 4

N_ATTN_HEADS = 8
HEAD_DIM = 64
ATTN_WIDTH = N_ATTN_HEADS * HEAD_DIM
DILATED_PATTERNS = ((128, 1), (512, 4), (2048, 16))
SPAN = 128
MAX_DILATION = 16
PAD_MULT = MAX_DILATION * SPAN
LRU_WIDTH = D_MODEL // 2
N_LRU_BLOCKS = 8
LRU_BLOCK = LRU_WIDTH // N_LRU_BLOCKS
CONV_WIDTH = 4
LRU_C = 8.0
MIX_WIDTH = ATTN_WIDTH + LRU_WIDTH
IN_WIDTH = 3 * ATTN_WIDTH + 2 * LRU_WIDTH
D_FF = 4 * D_MODEL
NORM_EPS = 1e-6

kernel_name = "hybrid_dilated_attn_rglru_block"


def rms_norm(x, g):
    xf = x.astype(jnp.float32)
    y = xf * lax.rsqrt(jnp.mean(xf * xf, axis=-1, keepdims=True) + NORM_EPS)
    return (y * g.astype(jnp.float32)).astype(x.dtype)


def dilated_branch(q, k, v, window, dilation):
    b, sp, h, dh = q.shape
    span = window // dilation
    nb = sp // (dilation * span)

    def split(t):
        return t.reshape(b, nb, span, dilation, h, dh).transpose(0, 3, 4, 1, 2, 5)

    def with_prev(t):
        prev = jnp.pad(t, ((0, 0), (0, 0), (0, 0), (1, 0), (0, 0), (0, 0)))[:, :, :, :-1]
        return jnp.concatenate([prev, t], axis=4)

    qb = split(q)
    kb = with_prev(split(k))
    vb = with_prev(split(v))
    s = jnp.einsum('bchnqd,bchnkd->bchnqk', qb, kb,
                   preferred_element_type=jnp.float32) * (HEAD_DIM ** -0.5)
    qi = jnp.arange(span)[:, None]
    kj = jnp.arange(2 * span)[None, :]
    dist = span + qi - kj
    blk = jnp.arange(nb)[:, None, None]
    valid = (dist >= 0) & (dist <= span) & ((blk > 0) | (kj >= span))
    s = jnp.where(valid, s, -jnp.inf)
    mx = jnp.max(s, axis=-1, keepdims=True)
    p = jnp.exp(s - mx)
    den = jnp.sum(p, axis=-1)
    o = jnp.einsum('bchnqk,bchnkd->bchnqd', p, vb.astype(jnp.float32)) / den[..., None]
    lse = mx[..., 0] + jnp.log(den)
    o = o.transpose(0, 3, 4, 1, 2, 5).reshape(b, sp, h, dh)
    lse = lse.transpose(0, 3, 4, 1, 2).reshape(b, sp, h)
    return o, lse


def dilated_attention(q, k, v):
    b, s, h, dh = q.shape
    sp = -(-s // PAD_MULT) * PAD_MULT
    pad = ((0, 0), (0, sp - s), (0, 0), (0, 0))
    qp, kp, vp = jnp.pad(q, pad), jnp.pad(k, pad), jnp.pad(v, pad)
    outs, lses = [], []
    for window, dilation in DILATED_PATTERNS:
        o, l = dilated_branch(qp, kp, vp, window, dilation)
        outs.append(o)
        lses.append(l)
    wts = jax.nn.softmax(jnp.stack(lses, axis=0), axis=0)
    o = jnp.einsum('pbsh,pbshd->bshd', wts, jnp.stack(outs, axis=0))
    return o[:, :s].astype(q.dtype)


def causal_depthwise_conv(x, w, bias):
    c = x.shape[-1]
    y = lax.conv_general_dilated(x, w[:, None, :].astype(x.dtype), window_strides=(1,),
                                 padding=[(CONV_WIDTH - 1, 0)],
                                 dimension_numbers=('NWC', 'WIO', 'NWC'),
                                 feature_group_count=c)
    return y + bias.astype(x.dtype)


def rg_lru(x, w_r, b_r, w_i, b_i, lam):
    b, s, c = x.shape
    xf = x.astype(jnp.float32)
    xb = xf.reshape(b, s, N_LRU_BLOCKS, LRU_BLOCK)
    r = jax.nn.sigmoid(jnp.einsum('bsnc,ncd->bsnd', xb, w_r.astype(jnp.float32)).reshape(b, s, c)
                       + b_r.astype(jnp.float32))
    i = jax.nn.sigmoid(jnp.einsum('bsnc,ncd->bsnd', xb, w_i.astype(jnp.float32)).reshape(b, s, c)
                       + b_i.astype(jnp.float32))
    log_a = -LRU_C * r * jax.nn.softplus(-lam.astype(jnp.float32))
    a = jnp.exp(log_a)
    u = jnp.sqrt(-jnp.expm1(2.0 * log_a)) * (i * xf)

    def combine(left, right):
        a_l, h_l = left
        a_r, h_r = right
        return a_l * a_r, a_r * h_l + h_r

    _, h = lax.associative_scan(combine, (a, u), axis=1)
    return h.astype(x.dtype)


def hybrid_mixer(h, w_in, conv_w, conv_b, w_r, b_r, w_i, b_i, lam, w_out):
    b, s, _ = h.shape
    z = h @ w_in
    q, k, v, xr, gr = jnp.split(
        z, [ATTN_WIDTH, 2 * ATTN_WIDTH, 3 * ATTN_WIDTH, 3 * ATTN_WIDTH + LRU_WIDTH], axis=-1)
    shp = (b, s, N_ATTN_HEADS, HEAD_DIM)
    attn = dilated_attention(q.reshape(shp), k.reshape(shp), v.reshape(shp)).reshape(b, s, ATTN_WIDTH)
    xr = causal_depthwise_conv(xr, conv_w, conv_b)
    rec = rg_lru(xr, w_r, b_r, w_i, b_i, lam) * jax.nn.gelu(gr)
    return jnp.concatenate([attn.astype(h.dtype), rec.astype(h.dtype)], axis=-1) @ w_out


def squared_relu_mlp(h, w1, w2):
    a = jax.nn.relu(h @ w1)
    return (a * a) @ w2


def setup_inputs(seed: int = 0) -> dict:
    key = jax.random.key(seed)
    ks = jax.random.split(key, 20)
    f32 = jnp.float32

    def gain(k):
        return 1.0 + 0.05 * jax.random.normal(k, (DEPTH, D_MODEL), f32)

    u = jax.random.uniform(ks[10], (DEPTH, LRU_WIDTH), f32, 0.9, 0.999)
    a0 = u ** (1.0 / LRU_C)
    lru_lambda = jnp.log(a0) - jnp.log1p(-a0)
    return {
        "x": jax.random.normal(ks[0], (BATCH, SEQ, D_MODEL), f32),
        "mix_norm_pre": gain(ks[1]),
        "mix_norm_post": gain(ks[2]),
        "mlp_norm_pre": gain(ks[3]),
        "mlp_norm_post": gain(ks[4]),
        "w_in": jax.random.normal(ks[5], (DEPTH, D_MODEL, IN_WIDTH), f32) * D_MODEL ** -0.5,
        "conv_w": jax.random.normal(ks[6], (DEPTH, CONV_WIDTH, LRU_WIDTH), f32) * CONV_WIDTH ** -0.5,
        "conv_b": 0.01 * jax.random.normal(ks[7], (DEPTH, LRU_WIDTH), f32),
        "w_rgate": jax.random.normal(ks[8], (DEPTH, N_LRU_BLOCKS, LRU_BLOCK, LRU_BLOCK), f32) * LRU_BLOCK ** -0.5,
        "b_rgate": 0.01 * jax.random.normal(ks[9], (DEPTH, LRU_WIDTH), f32),
        "w_igate": jax.random.normal(ks[11], (DEPTH, N_LRU_BLOCKS, LRU_BLOCK, LRU_BLOCK), f32) * LRU_BLOCK ** -0.5,
        "b_igate": 0.01 * jax.random.normal(ks[12], (DEPTH, LRU_WIDTH), f32),
        "lru_lambda": lru_lambda,
        "w_out": jax.random.normal(ks[13], (DEPTH, MIX_WIDTH, D_MODEL), f32) * MIX_WIDTH ** -0.5,
        "w_ff_in": jax.random.normal(ks[14], (DEPTH, D_MODEL, D_FF), f32) * D_MODEL ** -0.5,
        "w_ff_out": jax.random.normal(ks[15], (DEPTH, D_FF, D_MODEL), f32) * D_FF ** -0.5,
    }


def reference(x, mix_norm_pre, mix_norm_post, mlp_norm_pre, mlp_norm_post, w_in, conv_w, conv_b,
              w_rgate, b_rgate, w_igate, b_igate, lru_lambda, w_out, w_ff_in, w_ff_out):
    for l in range(DEPTH):
        h = rms_norm(x, mix_norm_pre[l])
        m = hybrid_mixer(h, w_in[l], conv_w[l], conv_b[l], w_rgate[l], b_rgate[l],
                         w_igate[l], b_igate[l], lru_lambda[l], w_out[l])
        x = x + rms_norm(m, mix_norm_post[l])
        h = rms_norm(x, mlp_norm_pre[l])
        x = x + rms_norm(squared_relu_mlp(h, w_ff_in[l], w_ff_out[l]), mlp_norm_post[l])
    return x
```

```python
import numpy as np
import ml_dtypes
from contextlib import ExitStack
import concourse.bass as bass
import concourse.mybir as mybir
from concourse.bass_utils import run_bass_kernel_spmd

F32 = mybir.dt.float32
BF16 = mybir.dt.bfloat16
AF = mybir.ActivationFunctionType
ALU = mybir.AluOpType

T = 4096
D = 1024
DEPTH = 4
NCORES = 8
EPS = 1e-6
ENGS = ("sync", "scalar", "vector", "gpsimd", "tensor")
GELU_C = 1.5957691216057308
DBG = {"p2": 9}


class Buf:
    __slots__ = ("name", "w", "r")

    def __init__(self, name):
        self.name = name
        self.w = {}
        self.r = {}


class DSem:
    __slots__ = ("key", "n")

    def __init__(self, key):
        self.key = key
        self.n = 0


class Prog:
    def __init__(self, nc):
        self.nc = nc
        self.es = ExitStack()
        self.q = {e: [] for e in ENGS}
        self.semh = {}
        self.cnt = {}
        self.latest = {}
        self.pend = {e: [] for e in ENGS}
        self.waited = {e: {} for e in ENGS}
        for e in ("scalar", "vector", "gpsimd", "tensor"):
            self.semh[e] = self.es.enter_context(nc.semaphore("prog_" + e))
            self.cnt[e] = 0
        self.nds = 0
        self.dsem_cache = {}
        self.dsem_seq = {}
        self.dsem_tag = 'glob'

    def dsem(self, name):
        self.nds += 1
        idx = self.dsem_seq.get(self.dsem_tag, 0)
        self.dsem_seq[self.dsem_tag] = idx + 1
        ck = (self.dsem_tag, idx)
        if ck in self.dsem_cache:
            return self.dsem_cache[ck]
        key = "d%d_%s" % (self.nds, name)
        self.semh[key] = self.es.enter_context(self.nc.semaphore(key))
        ds = DSem(key)
        self.dsem_cache[ck] = ds
        return ds

    def set_tag(self, tag):
        self.dsem_tag = tag
        self.dsem_seq[tag] = 0

    def _need(self, eng, key, val):
        if self.waited[eng].get(key, 0) >= val:
            return
        self.waited[eng][key] = val
        self.q[eng].append(("w", key, val))

    def _hazards(self, eng, mykey, reads, writes):
        for b in reads:
            for k, v in b.w.items():
                self._need(eng, k, v)
        for b in writes:
            for k, v in b.w.items():
                if k != mykey:
                    self._need(eng, k, v)
            for k, v in b.r.items():
                if k != mykey:
                    self._need(eng, k, v)

    def _commit(self, key, val, reads, writes):
        for b in reads:
            if b.r.get(key, 0) < val:
                b.r[key] = val
        for b in writes:
            b.w = {key: val}
            b.r = {}
        self.latest[key] = val

    def op(self, eng, fn, reads=(), writes=(), sig=True):
        self._hazards(eng, eng, reads, writes)
        if sig:
            self.cnt[eng] += 1
            val = self.cnt[eng]
            self.q[eng].append(("i", fn, eng, 1))
            for (r, w) in self.pend[eng]:
                self._commit(eng, val, r, w)
            self.pend[eng] = []
            self._commit(eng, val, reads, writes)
        else:
            self.q[eng].append(("i", fn, None, 0))
            self.pend[eng].append((reads, writes))

    def dma(self, eng, out, in_, ds, reads=(), writes=()):
        self._hazards(eng, ds.key, reads, writes)
        ds.n += 16
        self.q[eng].append(("i", lambda e, o=out, i=in_: e.dma_start(out=o, in_=i), ds.key, 16))
        self._commit(ds.key, ds.n, reads, writes)

    def barrier(self):
        for e in ENGS:
            assert not self.pend[e]
            for key, val in self.latest.items():
                if key != e:
                    self._need(e, key, val)

    def flush(self):
        if not any(self.q[e] for e in ENGS):
            return
        with self.nc.Block() as blk:
            for name in ENGS:
                items = self.q[name]
                if not items:
                    continue

                def body(e, items=items):
                    for it in items:
                        if it[0] == "w":
                            e.wait_ge(self.semh[it[1]], it[2])
                        else:
                            ins = it[1](e)
                            if it[2] is not None:
                                ins.then_inc(self.semh[it[2]], it[3])

                getattr(blk, name)(body)
        self.q = {e: [] for e in ENGS}

    def act(self, out, in_, func, reads, writes, **kw):
        self.op("scalar", lambda e: e.activation(out=out, in_=in_, func=func, **kw), reads, writes)

    def acopy(self, out, in_, reads, writes):
        self.op("scalar", lambda e: e.copy(out=out, in_=in_), reads, writes)

    def tcopy(self, eng, out, in_, reads, writes):
        self.op(eng, lambda e: e.tensor_copy(out=out, in_=in_), reads, writes)

    def tt(self, eng, out, in0, in1, op, reads, writes):
        self.op(eng, lambda e: e.tensor_tensor(out=out, in0=in0, in1=in1, op=op), reads, writes)

    def ts(self, eng, out, in0, s1, s2, op0, op1, reads, writes):
        if op1 is None:
            self.op(eng, lambda e: e.tensor_scalar(out=out, in0=in0, scalar1=s1, scalar2=None, op0=op0),
                    reads, writes)
        else:
            self.op(eng, lambda e: e.tensor_scalar(out=out, in0=in0, scalar1=s1, scalar2=s2, op0=op0, op1=op1),
                    reads, writes)

    def stt(self, out, in0, scalar, in1, op0, op1, reads, writes):
        self.op("vector", lambda e: e.scalar_tensor_tensor(out=out, in0=in0, scalar=scalar, in1=in1,
                                                            op0=op0, op1=op1), reads, writes)

    def mm(self, out, lhsT, rhs, start, stop, reads=(), writes=(), sig=True):
        self.op("tensor", lambda e: e.matmul(out, lhsT, rhs, start=start, stop=stop), reads, writes, sig=sig)

    def tr(self, out, in_, ident, reads=(), writes=(), sig=True):
        self.op("tensor", lambda e: e.transpose(out, in_, ident), reads, writes, sig=sig)


class Ctx:
    pass


def sl(start, count, step):
    return slice(start, start + (count - 1) * step + 1, step)


def _scope(nc, es, tag):
    def sb(name, shape, dtype):
        return es.enter_context(nc.sbuf_tensor("%s_%s" % (name, tag), shape, dtype))

    def ps(name, shape, dtype):
        return es.enter_context(nc.psum_tensor("%s_%s" % (name, tag), shape, dtype))

    return sb, ps


def _rstd(P, C, ss_ap, ms_ap, rstd_ap, n, reads, writes_ms, writes_rstd):
    P.ts("gpsimd", ms_ap, ss_ap, 1.0 / D, EPS, ALU.mult, ALU.add, reads, writes_ms)
    P.tt("gpsimd", rstd_ap, ms_ap, C.neghalf[:, 0:n], ALU.pow, writes_ms, writes_rstd)


def phase1(P, C, l, xsrc):
    nc = P.nc
    NT = T // 512
    with ExitStack() as es:
        sb, ps = _scope(nc, es, "p1l%d" % l)
        win = sb("win", [128, 8, 2560], BF16)
        wst = [sb("wst%d" % i, [128, 2560], F32) for i in range(2)]
        gbc = sb("gbc", [128, D], F32)
        xt = [sb("xt%d" % i, [128, 4, D], F32) for i in range(2)]
        hb = [sb("hb%d" % i, [128, 4, D], BF16) for i in range(2)]
        sq = sb("sq", [128, D], BF16)
        ss = [sb("ss%d" % i, [128, 4], F32) for i in range(2)]
        ms = [sb("ms%d" % i, [128, 4], F32) for i in range(2)]
        rstd = [sb("rstd%d" % i, [128, 4], F32) for i in range(2)]
        hT = [sb("hT%d" % i, [128, 8, 512], BF16) for i in range(2)]
        zq = [sb("zq%d" % i, [128, 12, 512], BF16) for i in range(2)]
        zl = [sb("zl%d" % i, [128, 8, 512], F32) for i in range(2)]
        tp = [ps("tp%d" % i, [128, 8, 128], BF16) for i in range(2)]
        mmp = [ps("mm%d" % i, [128, 512], F32) for i in range(4)]

        win_b = [Buf("win%d" % k) for k in range(8)]
        wst_b = [Buf("wst") for _ in range(2)]
        gbc_b = Buf("gbc")
        xt_b = [Buf("xt") for _ in range(2)]
        hb_b = [Buf("hb") for _ in range(2)]
        ss_b = [Buf("ss") for _ in range(2)]
        ms_b = [Buf("ms") for _ in range(2)]
        rstd_b = [Buf("rstd") for _ in range(2)]
        hT_b = [Buf("hT") for _ in range(2)]
        zq_b = [[Buf("zq") for _ in range(12)] for _ in range(2)]
        zl_b = [[Buf("zl") for _ in range(8)] for _ in range(2)]
        tp_b = [Buf("tp") for _ in range(2)]
        mm_b = [Buf("mm") for _ in range(4)]
        d_w = [P.dsem("w") for _ in range(2)]
        d_g = P.dsem("g")
        d_x = [P.dsem("x") for _ in range(2)]
        d_sq = [P.dsem("sq") for _ in range(2)]
        d_sl = [P.dsem("sl") for _ in range(2)]

        cast_eng = ("gpsimd", "vector", "scalar")
        for k in range(8):
            P.dma("sync", wst[k % 2][:], C.w_in[l, k * 128:(k + 1) * 128, :], d_w[k % 2], (), (wst_b[k % 2],))
            eng = cast_eng[k % 3]
            if eng == "scalar":
                P.acopy(win[:, k, :], wst[k % 2][:], (wst_b[k % 2],), (win_b[k],))
            else:
                P.tcopy(eng, win[:, k, :], wst[k % 2][:], (wst_b[k % 2],), (win_b[k],))
        P.dma("sync", gbc[:], C.mix_norm_pre[l:l + 1, :].partition_broadcast(128), d_g, (), (gbc_b,))

        def load(i):
            b = i % 2
            src = xsrc[i * 512:(i + 1) * 512, :].rearrange("(s p) d -> p s d", p=128)
            P.dma("sync", xt[b][:], src, d_x[b], (C.xres_b,), (xt_b[b],))

        def norm(i):
            b = i % 2
            for s in range(4):
                P.act(sq[:], xt[b][:, s, :], AF.Square, (xt_b[b],), (ss_b[b],), accum_out=ss[b][:, s:s + 1])
            _rstd(P, C, ss[b][:], ms[b][:], rstd[b][:], 4, (ss_b[b],), (ms_b[b],), (rstd_b[b],))
            for s in range(4):
                P.stt(hb[b][:, s, :], xt[b][:, s, :], rstd[b][:, s:s + 1], gbc[:], ALU.mult, ALU.mult,
                      (xt_b[b], rstd_b[b], gbc_b), (hb_b[b],))

        tpc = [0]

        def transp(i):
            b = i % 2
            for s in range(4):
                pb = tpc[0] % 2
                tpc[0] += 1
                for k in range(8):
                    P.tr(tp[pb][:, k, :], hb[b][:, s, k * 128:(k + 1) * 128], C.ident[:],
                         (hb_b[b], C.ident_b), (tp_b[pb],), sig=(k == 7))
                P.acopy(hT[b][:, :, s * 128:(s + 1) * 128], tp[pb][:], (tp_b[pb],), (hT_b[b],))

        def matmuls(i):
            b = i % 2
            for c in range(20):
                mb = c % 4
                for k in range(8):
                    P.mm(mmp[mb][:], win[:, k, c * 128:(c + 1) * 128], hT[b][:, k, :], k == 0, k == 7,
                         (hT_b[b], win_b[k]), (mm_b[mb],), sig=(k == 7))
                if c < 12:
                    dst, dbuf = zq[b][:, c, :], zq_b[b][c]
                else:
                    dst, dbuf = zl[b][:, c - 12, :], zl_b[b][c - 12]
                if mb % 2 == 0:
                    P.acopy(dst, mmp[mb][:], (mm_b[mb],), (dbuf,))
                else:
                    P.tcopy("vector", dst, mmp[mb][:], (mm_b[mb],), (dbuf,))

        def store(i):
            b = i % 2
            dq = C.zqkv.rearrange("(c p) t -> p c t", p=128)[:, :, i * 512:(i + 1) * 512]
            dl = C.zlru.rearrange("(c p) t -> p c t", p=128)[:, :, i * 512:(i + 1) * 512]
            P.dma("sync", dq, zq[b][:], d_sq[b], tuple(zq_b[b]), (C.zqkv_b,))
            P.dma("sync", dl, zl[b][:], d_sl[b], tuple(zl_b[b]), (C.zlru_b,))

        load(0)
        load(1)
        norm(0)
        transp(0)
        for i in range(NT):
            if i + 1 < NT:
                norm(i + 1)
                transp(i + 1)
            if i + 2 < NT:
                load(i + 2)
            matmuls(i)
            store(i)
        P.barrier()
        P.flush()


def phase2(P, C, l):
    nc = P.nc
    with ExitStack() as es:
        sb, ps = _scope(nc, es, "p2l%d" % l)
        qT = [sb("qT%d" % i, [128, T], BF16) for i in range(2)]
        vT = [sb("vT%d" % i, [128, T], BF16) for i in range(2)]
        kz = [[sb("kz%d_%d" % (i, h), [128, T], BF16) for h in range(2)] for i in range(2)]
        Vd = [sb("Vd%d" % i, [128, 32, 2, 128], BF16) for i in range(2)]
        acc = [sb("acc%d" % i, [128, T], F32) for i in range(2)]
        PT = [sb("PT%d" % i, [128, 2, 256], BF16) for i in range(4)]
        kzp = [[sb("kzp%d_%d" % (i, h), [128, T], BF16) for h in range(2)] for i in range(2)]
        vTp = sb("vTp", [128, T], BF16)
        rc = [sb("rc%d" % i, [128, 512], F32) for i in range(2)]
        rs = [sb("rs%d" % i, [128, 512], F32) for i in range(2)]
        ao = [sb("ao%d" % i, [128, T], BF16) for i in range(2)]
        sp = [ps("sp%d" % i, [128, 2, 256], F32) for i in range(2)]
        opp = [ps("op%d" % i, [128, 512], F32) for i in range(2)]
        vp = [ps("vp%d" % i, [128, 8, 128], BF16) for i in range(2)]

        qT_b = [Buf("qT") for _ in range(2)]
        vT_b = [Buf("vT") for _ in range(2)]
        kz_b = [[Buf("kz") for _ in range(2)] for _ in range(2)]
        Vd_b = [[Buf("Vd") for _ in range(2)] for _ in range(2)]
        vst = [sb("vst%d" % i, [128, 8, 128], BF16) for i in range(2)]
        vst_b = [Buf("vst") for _ in range(2)]
        kzp_b = [[Buf("kzp") for _ in range(2)] for _ in range(2)]
        vTp_b = Buf("vTp")
        acc_b = [Buf("acc") for _ in range(2)]
        PT_b = [Buf("PT") for _ in range(4)]
        rc_b = [Buf("rc") for _ in range(2)]
        rs_b = [Buf("rs") for _ in range(2)]
        ao_b = [Buf("ao") for _ in range(2)]
        sp_b = [Buf("sp") for _ in range(2)]
        op_b = [Buf("op") for _ in range(2)]
        vp_b = [Buf("vp") for _ in range(2)]
        d_q = [P.dsem("q") for _ in range(2)]
        d_v = [P.dsem("v") for _ in range(2)]
        d_k = [[P.dsem("k") for _ in range(2)] for _ in range(2)]
        d_o = [P.dsem("o") for _ in range(2)]

        for i in range(2):
            P.op("vector", lambda e, t=kz[i][0]: e.memset(t[:], 0.0), (), (kz_b[i][0],))
            P.op("vector", lambda e, t=kz[i][1]: e.memset(t[:], 0.0), (), (kz_b[i][1],))
            P.op("vector", lambda e, t=Vd[i]: e.memset(t[:, :, 0, 64:128], 1.0), (), (Vd_b[i][0],))
            P.op("vector", lambda e, t=Vd[i]: e.memset(t[:, :, 1, 0:64], 1.0), (), (Vd_b[i][1],))

        def load(j):
            b = j % 2
            P.dma("sync", qT[b][:], C.zqkv[j * 128:(j + 1) * 128, :], d_q[b], (C.zqkv_b,), (qT_b[b],))
            P.dma("sync", kz[b][0][0:64, :], C.zqkv[512 + j * 128:512 + j * 128 + 64, :], d_k[b][0],
                  (C.zqkv_b,), (kz_b[b][0],))
            P.dma("sync", kz[b][1][64:128, :], C.zqkv[512 + j * 128 + 64:512 + (j + 1) * 128, :], d_k[b][1],
                  (C.zqkv_b,), (kz_b[b][1],))
            P.dma("sync", vT[b][:], C.zqkv[1024 + j * 128:1024 + (j + 1) * 128, :], d_v[b], (C.zqkv_b,), (vT_b[b],))

        units = [(j, d) for j in range(4) for d in (1, 4, 16)][:DBG.get("nunits", 12)]
        cnt = dict(vp=0, sp=0, pt=0, op=0, nr=0)

        def vbuild(u):
            if DBG["p2"] < 1:
                return
            j, d = units[u]
            b = j % 2
            vb = u % 2
            nb = 32 // d
            if d == 1:
                vsrc, vsrc_b = vT[b], vT_b[b]
            else:
                kb = u % 2
                for hh in range(2):
                    P.tcopy("gpsimd", kzp[kb][hh][:].rearrange("p (c u) -> p c u", c=d),
                            kz[b][hh][:].rearrange("p (u c) -> p c u", c=d),
                            (kz_b[b][hh],), (kzp_b[kb][hh],))
                P.tcopy("gpsimd", vTp[:].rearrange("p (c u) -> p c u", c=d),
                        vT[b][:].rearrange("p (u c) -> p c u", c=d), (vT_b[b],), (vTp_b,))
                vsrc, vsrc_b = vTp, vTp_b
                P.flush()
            for g8 in range(4):
                pb = cnt["vp"] % 2
                cnt["vp"] += 1
                for s in range(8):
                    bi = g8 * 8 + s
                    P.tr(vp[pb][:, s, :], vsrc[:, bi * 128:(bi + 1) * 128], C.ident[:],
                         (vsrc_b, C.ident_b), (vp_b[pb],), sig=(s == 7))
                P.acopy(vst[pb][:], vp[pb][:], (vp_b[pb],), (vst_b[pb],))
                P.tcopy("vector", Vd[vb][:, g8 * 8:(g8 + 1) * 8, 0, 0:64], vst[pb][:, :, 0:64],
                        (vst_b[pb],), (Vd_b[vb][0],))
                P.tcopy("vector", Vd[vb][:, g8 * 8:(g8 + 1) * 8, 1, 64:128], vst[pb][:, :, 64:128],
                        (vst_b[pb],), (Vd_b[vb][1],))

        def groups_of(u):
            j, d = units[u]
            nb = 32 // d
            out = []
            for hh in range(2):
                for c in range(d):
                    for g in range(nb // 2):
                        out.append((u, hh, c, g))
            return out

        allgroups = []
        for u in range(len(units)):
            allgroups.append(groups_of(u))

        def emit_S(grp, gi):
            if DBG["p2"] < 2:
                return
            u, hh, c, g = grp
            j, d = units[u]
            b = j % 2
            nb = 32 // d
            sbk = gi % 2
            for mi, m in enumerate((2 * g, 2 * g + 1)):
                nq = 256 if m + 1 < nb else 128
                qs = 128 * m * d + c
                P.mm(sp[sbk][:, mi, 0:nq], C.ident[:], C.maskb[:, 0:nq], True, False,
                     (C.ident_b, C.maskb_b), (sp_b[sbk],), sig=False)
                bi = c * nb + m
                if d == 1:
                    ksrc, ksrc_b = kz[b][hh], kz_b[b][hh]
                else:
                    ksrc, ksrc_b = kzp[u % 2][hh], kzp_b[u % 2][hh]
                P.mm(sp[sbk][:, mi, 0:nq], ksrc[:, bi * 128:(bi + 1) * 128], qT[b][:, sl(qs, nq, d)],
                     False, True, (ksrc_b, qT_b[b]), (sp_b[sbk],), sig=(mi == 1))

        def emit_exp(grp, gi):
            if DBG["p2"] < 2:
                return
            sbk = gi % 2
            pt = gi % 4
            P.act(PT[pt][:], sp[sbk][:], AF.Exp, (sp_b[sbk],), (PT_b[pt],), scale=0.125)

        ostate = dict(ob=0, nslot=0, n0=0)

        def emit_PV(grp, gi):
            if DBG["p2"] < 3:
                return
            u, hh, c, g = grp
            j, d = units[u]
            vb = u % 2
            nb = 32 // d
            pt = gi % 4
            ptp = (gi - 1) % 4

            def bidx(m):
                return c * nb + m

            for n in (2 * g, 2 * g + 1):
                if ostate["nslot"] == 0:
                    ostate["n0"] = n
                ob = ostate["ob"]
                slot = ostate["nslot"]
                dst = opp[ob][:, slot * 128:(slot + 1) * 128]
                if n == 2 * g:
                    if g > 0:
                        P.mm(dst, Vd[vb][:, bidx(n - 1), hh, :], PT[ptp][:, 1, 128:256], True, False,
                             (Vd_b[vb][hh], PT_b[ptp]), (op_b[ob],), sig=False)
                    P.mm(dst, Vd[vb][:, bidx(n), hh, :], PT[pt][:, 0, 0:128], g == 0, True,
                         (Vd_b[vb][hh], PT_b[pt]), (op_b[ob],), sig=True)
                else:
                    P.mm(dst, Vd[vb][:, bidx(n - 1), hh, :], PT[pt][:, 0, 128:256], True, False,
                         (Vd_b[vb][hh], PT_b[pt]), (op_b[ob],), sig=False)
                    P.mm(dst, Vd[vb][:, bidx(n), hh, :], PT[pt][:, 1, 0:128], False, True,
                         (Vd_b[vb][hh], PT_b[pt]), (op_b[ob],), sig=True)
                ostate["nslot"] += 1
                if ostate["nslot"] == 4 or n == nb - 1:
                    ns = ostate["nslot"]
                    n0 = ostate["n0"]
                    st = 128 * n0 * d + c
                    av = acc[hh][:, sl(st, ns * 128, d)]
                    src = opp[ob][:, 0:ns * 128]
                    if d == 1:
                        P.tcopy("vector", av, src, (op_b[ob],), (acc_b[hh],))
                    else:
                        P.tt("vector", av, av, src, ALU.add, (op_b[ob], acc_b[hh]), (acc_b[hh],))
                    ostate["nslot"] = 0
                    ostate["ob"] = 1 - ob

        def normalize(j):
            if DBG["p2"] < 4:
                return
            ab = j % 2
            for hh in range(2):
                nlo, dlo = (0, 64) if hh == 0 else (64, 0)
                for tcn in range(8):
                    x = cnt["nr"] % 2
                    cnt["nr"] += 1
                    tsl = slice(tcn * 512, (tcn + 1) * 512)
                    P.op("vector", lambda e, o=rc[x][dlo:dlo + 64, :], i=acc[hh][dlo:dlo + 64, tsl]:
                         e.reciprocal(out=o, in_=i), (acc_b[hh],), (rc_b[x],))
                    P.acopy(rs[x][nlo:nlo + 64, :], rc[x][dlo:dlo + 64, :], (rc_b[x],), (rs_b[x],))
                    P.tt("vector", ao[ab][nlo:nlo + 64, tsl], acc[hh][nlo:nlo + 64, tsl], rs[x][nlo:nlo + 64, :],
                         ALU.mult, (acc_b[hh], rs_b[x]), (ao_b[ab],))
                P.flush()
            P.dma("sync", C.mixT[j * 128:(j + 1) * 128, :], ao[ab][:], d_o[ab], (ao_b[ab],), (C.mixT_b,))

        load(0)
        gbase = 0
        for u in range(len(units)):
            j, d = units[u]
            if d == 1 and j + 1 < 4:
                load(j + 1)
            vbuild(u)
            P.flush()
            flat = allgroups[u]
            emit_S(flat[0], gbase)
            for li, grp in enumerate(flat):
                gi = gbase + li
                if li + 1 < len(flat):
                    emit_S(flat[li + 1], gi + 1)
                emit_exp(grp, gi)
                emit_PV(grp, gi)
                P.flush()
            gbase += len(flat)
            P.flush()
            if d == 16:
                normalize(j)
                P.flush()
        P.barrier()
        P.flush()


def phase3(P, C, l):
    nc = P.nc
    N = 512
    NTT = T // N
    with ExitStack() as es:
        sb, ps = _scope(nc, es, "p3l%d" % l)
        wr = sb("wr", [128, 4, 128], F32)
        wi = sb("wi", [128, 4, 128], F32)
        xr = [sb("xr%d" % i, [128, 3 + N], F32) for i in range(2)]
        gr = [sb("gr%d" % i, [128, N], F32) for i in range(2)]
        hs = [sb("hs%d" % i, [128, N], F32) for i in range(2)]
        ro = [sb("ro%d" % i, [128, N], BF16) for i in range(2)]
        names = ["y", "er", "ei", "t1", "t2", "r", "ig", "a", "a2", "lnm", "mlt", "u1", "u",
                 "g2", "g3", "g4", "ge", "gp", "gs", "gate"]
        tmp = [{n: sb("%s%d" % (n, i), [128, N], F32) for n in names} for i in range(2)]
        tb = [{n: Buf(n) for n in names} for _ in range(2)]
        pr = [ps("pr%d" % i, [128, N], F32) for i in range(2)]
        pi = [ps("pi%d" % i, [128, N], F32) for i in range(2)]
        wr_b, wi_b = Buf("wr"), Buf("wi")
        xr_b = [Buf("xr") for _ in range(2)]
        gr_b = [Buf("gr") for _ in range(2)]
        hs_b = [Buf("hs") for _ in range(2)]
        ro_b = [Buf("ro") for _ in range(2)]
        pr_b = [Buf("pr") for _ in range(2)]
        pi_b = [Buf("pi") for _ in range(2)]
        d_wr, d_wi = P.dsem("wr"), P.dsem("wi")
        d_xr = [P.dsem("xr") for _ in range(2)]
        d_gr = [P.dsem("gr") for _ in range(2)]
        d_ro = [P.dsem("ro") for _ in range(2)]

        P.op("gpsimd", lambda e: e.memset(wr[:], 0.0), (), (wr_b,))
        P.op("gpsimd", lambda e: e.memset(wi[:], 0.0), (), (wi_b,))
        for j in range(4):
            for h in range(2):
                P.dma("sync", wr[h * 64:(h + 1) * 64, j, h * 64:(h + 1) * 64], C.w_rgate[l, 2 * j + h], d_wr,
                      (), (wr_b,))
                P.dma("sync", wi[h * 64:(h + 1) * 64, j, h * 64:(h + 1) * 64], C.w_igate[l, 2 * j + h], d_wi,
                      (), (wi_b,))

        tiles = [(j, tt) for j in range(4) for tt in range(NTT)]

        def col(tab, j, k):
            o = l * 32 + j * 8 + k
            return tab[:, o:o + 1]

        def load(idx):
            j, tt = tiles[idx]
            b = idx % 2
            t0 = tt * N
            if tt == 0:
                P.op("gpsimd", lambda e, t=xr[b]: e.memset(t[:, 0:3], 0.0), (), (xr_b[b],))
                P.dma("sync", xr[b][:, 3:3 + N], C.zlru[j * 128:(j + 1) * 128, 0:N], d_xr[b], (C.zlru_b,), (xr_b[b],))
            else:
                P.dma("sync", xr[b][:, 0:3 + N], C.zlru[j * 128:(j + 1) * 128, t0 - 3:t0 + N], d_xr[b],
                      (C.zlru_b,), (xr_b[b],))
            P.dma("sync", gr[b][:], C.zlru[512 + j * 128:512 + (j + 1) * 128, t0:t0 + N], d_gr[b],
                  (C.zlru_b,), (gr_b[b],))

        def compute(idx):
            j, tt = tiles[idx]
            b = idx % 2
            t0 = tt * N
            tm, bb = tmp[b], tb[b]
            X = xr[b]
            P.ts("vector", tm["y"][:], X[:, 3:3 + N], col(C.lv, j, 3), col(C.lv, j, 4), ALU.mult, ALU.add,
                 (xr_b[b], C.lv_b), (bb["y"],))
            for k in (2, 1, 0):
                P.stt(tm["y"][:], X[:, k:k + N], col(C.lv, j, k), tm["y"][:], ALU.mult, ALU.add,
                      (xr_b[b], C.lv_b, bb["y"]), (bb["y"],))
            P.mm(pr[b][:], wr[:, j, :], tm["y"][:], True, True, (wr_b, bb["y"]), (pr_b[b],))
            P.mm(pi[b][:], wi[:, j, :], tm["y"][:], True, True, (wi_b, bb["y"]), (pi_b[b],))
            P.act(tm["er"][:], pr[b][:], AF.Exp, (pr_b[b], C.lvn_b), (bb["er"],), scale=-1.0, bias=col(C.lvn, j, 5))
            P.act(tm["ei"][:], pi[b][:], AF.Exp, (pi_b[b], C.lvn_b), (bb["ei"],), scale=-1.0, bias=col(C.lvn, j, 6))
            P.ts("gpsimd", tm["t1"][:], tm["er"][:], 1.0, None, ALU.add, None, (bb["er"],), (bb["t1"],))
            P.ts("gpsimd", tm["t2"][:], tm["ei"][:], 1.0, None, ALU.add, None, (bb["ei"],), (bb["t2"],))
            P.op("vector", lambda e, o=tm["r"][:], i=tm["t1"][:]: e.reciprocal(out=o, in_=i), (bb["t1"],), (bb["r"],))
            P.op("vector", lambda e, o=tm["ig"][:], i=tm["t2"][:]: e.reciprocal(out=o, in_=i), (bb["t2"],), (bb["ig"],))
            P.act(tm["a"][:], tm["r"][:], AF.Exp, (bb["r"], C.lc1_b), (bb["a"],), scale=col(C.lc1, j, 7))
            P.act(tm["a2"][:], tm["r"][:], AF.Exp, (bb["r"], C.lc2_b), (bb["a2"],), scale=col(C.lc2, j, 7))
            P.act(tm["lnm"][:], tm["a2"][:], AF.Ln, (bb["a2"],), (bb["lnm"],), scale=-1.0, bias=C.one[:, 0:1])
            P.act(tm["mlt"][:], tm["lnm"][:], AF.Exp, (bb["lnm"],), (bb["mlt"],), scale=0.5)
            P.tt("gpsimd", tm["u1"][:], tm["ig"][:], tm["y"][:], ALU.mult, (bb["ig"], bb["y"]), (bb["u1"],))
            P.tt("gpsimd", tm["u"][:], tm["u1"][:], tm["mlt"][:], ALU.mult, (bb["u1"], bb["mlt"]), (bb["u"],))
            if tt == 0:
                init = 0.0
                rd = (bb["a"], bb["u"])
            else:
                init = hs[1 - b][:, N - 1:N]
                rd = (bb["a"], bb["u"], hs_b[1 - b])
            P.op("vector", lambda e, o=hs[b][:], d0=tm["a"][:], d1=tm["u"][:], ini=init:
                 e.tensor_tensor_scan(out=o, data0=d0, data1=d1, initial=ini, op0=ALU.mult, op1=ALU.add),
                 rd, (hs_b[b],))
            G = gr[b]
            P.tt("gpsimd", tm["g2"][:], G[:], G[:], ALU.mult, (gr_b[b],), (bb["g2"],))
            P.ts("gpsimd", tm["g3"][:], tm["g2"][:], 0.044715, 1.0, ALU.mult, ALU.add, (bb["g2"],), (bb["g3"],))
            P.tt("gpsimd", tm["g4"][:], tm["g3"][:], G[:], ALU.mult, (bb["g3"], gr_b[b]), (bb["g4"],))
            P.act(tm["ge"][:], tm["g4"][:], AF.Exp, (bb["g4"],), (bb["ge"],), scale=-GELU_C)
            P.ts("gpsimd", tm["gp"][:], tm["ge"][:], 1.0, None, ALU.add, None, (bb["ge"],), (bb["gp"],))
            P.op("vector", lambda e, o=tm["gs"][:], i=tm["gp"][:]: e.reciprocal(out=o, in_=i), (bb["gp"],), (bb["gs"],))
            P.tt("gpsimd", tm["gate"][:], tm["gs"][:], G[:], ALU.mult, (bb["gs"], gr_b[b]), (bb["gate"],))
            P.tt("vector", ro[b][:], hs[b][:], tm["gate"][:], ALU.mult, (hs_b[b], bb["gate"]), (ro_b[b],))
            P.dma("sync", C.mixT[512 + j * 128:512 + (j + 1) * 128, t0:t0 + N], ro[b][:], d_ro[b],
                  (ro_b[b],), (C.mixT_b,))

        load(0)
        load(1)
        for idx in range(len(tiles)):
            compute(idx)
            if idx + 2 < len(tiles):
                load(idx + 2)
        P.barrier()
        P.flush()


def phase4(P, C, l, xsrc):
    nc = P.nc
    NS = 256
    NT = T // NS
    with ExitStack() as eso:
        sbo, pso = _scope(nc, eso, "p4l%d" % l)
        w1 = sbo("w1", [128, 8, 4096], BF16)
        w2 = sbo("w2", [128, 32, D], BF16)
        g2bc = sbo("g2bc", [128, D], F32)
        g3bc = sbo("g3bc", [128, D], F32)
        w1_b = [Buf("w1") for _ in range(8)]
        w2_b = [Buf("w2") for _ in range(32)]
        g2_b, g3_b = Buf("g2"), Buf("g3")
        d_g2, d_g3 = P.dsem("g2"), P.dsem("g3")
        cast_eng = ("gpsimd", "vector", "scalar")

        with ExitStack() as es:
            sb, ps = _scope(nc, es, "p4al%d" % l)
            wo = sb("wo", [128, 8, D], BF16)
            wst = [sb("wst%d" % i, [128, 1024], F32) for i in range(2)]
            gbc = sb("gbc", [128, D], F32)
            mx = [sb("mx%d" % i, [128, 8, NS], BF16) for i in range(2)]
            xt = [sb("xt%d" % i, [128, 2, D], F32) for i in range(2)]
            tmp = [sb("tmp%d" % i, [128, D], F32) for i in range(2)]
            sq = sb("sq", [128, D], BF16)
            ss = [sb("ss%d" % i, [128, 1], F32) for i in range(2)]
            ms = [sb("ms%d" % i, [128, 1], F32) for i in range(2)]
            rstd = [sb("rstd%d" % i, [128, 1], F32) for i in range(2)]
            pm = [ps("pm%d" % i, [128, D], F32) for i in range(2)]
            wo_b = [Buf("wo") for _ in range(8)]
            wst_b = [Buf("wst") for _ in range(2)]
            gbc_b = Buf("gbc")
            mx_b = [Buf("mx") for _ in range(2)]
            xt_b = [[Buf("xt") for _ in range(2)] for _ in range(2)]
            tmp_b = [Buf("tmp") for _ in range(2)]
            ss_b = [Buf("ss") for _ in range(2)]
            ms_b = [Buf("ms") for _ in range(2)]
            rstd_b = [Buf("rstd") for _ in range(2)]
            pm_b = [Buf("pm") for _ in range(2)]
            d_w = [P.dsem("w") for _ in range(2)]
            d_g = P.dsem("g")
            d_mx = [P.dsem("mx") for _ in range(2)]
            d_x = [P.dsem("x") for _ in range(2)]
            d_st = [P.dsem("st") for _ in range(2)]
            wc = [0]

            def wload(dst, src, dbuf, n):
                i = wc[0] % 2
                eng = cast_eng[wc[0] % 3]
                wc[0] += 1
                P.dma("sync", wst[i][:, 0:n], src, d_w[i], (), (wst_b[i],))
                if eng == "scalar":
                    P.acopy(dst, wst[i][:, 0:n], (wst_b[i],), (dbuf,))
                else:
                    P.tcopy(eng, dst, wst[i][:, 0:n], (wst_b[i],), (dbuf,))

            for k in range(8):
                wload(wo[:, k, :], C.w_out[l, k * 128:(k + 1) * 128, :], wo_b[k], D)
            P.dma("sync", gbc[:], C.mix_norm_post[l:l + 1, :].partition_broadcast(128), d_g, (), (gbc_b,))
            P.dma("sync", g2bc[:], C.mlp_norm_pre[l:l + 1, :].partition_broadcast(128), d_g2, (), (g2_b,))
            P.dma("sync", g3bc[:], C.mlp_norm_post[l:l + 1, :].partition_broadcast(128), d_g3, (), (g3_b,))

            witems = []
            for k in range(8):
                for hf in range(4):
                    witems.append((w1[:, k, hf * 1024:(hf + 1) * 1024],
                                   C.w_ff_in[l, k * 128:(k + 1) * 128, hf * 1024:(hf + 1) * 1024], w1_b[k], 1024))
            for kc in range(32):
                witems.append((w2[:, kc, :], C.w_ff_out[l, kc * 128:(kc + 1) * 128, :], w2_b[kc], 1024))

            def load(i):
                b = i % 2
                P.dma("sync", mx[b][:], C.mixT.rearrange("(c p) t -> p c t", p=128)[:, :, i * NS:(i + 1) * NS],
                      d_mx[b], (C.mixT_b,), (mx_b[b],))
                P.dma("sync", xt[b][:], xsrc[i * NS:(i + 1) * NS, :].rearrange("(s p) d -> p s d", p=128),
                      d_x[b], (C.xres_b,), (xt_b[b][0], xt_b[b][1]))

            ec = [0]

            def tile(i):
                b = i % 2
                for s in range(2):
                    pb = ec[0] % 2
                    ec[0] += 1
                    for nt in range(2):
                        for k in range(8):
                            P.mm(pm[pb][:, nt * 512:(nt + 1) * 512], mx[b][:, k, s * 128:(s + 1) * 128],
                                 wo[:, k, nt * 512:(nt + 1) * 512], k == 0, k == 7,
                                 (mx_b[b], wo_b[k]), (pm_b[pb],), sig=(k == 7 and nt == 1))
                    P.act(sq[:], pm[pb][:], AF.Square, (pm_b[pb],), (ss_b[pb],), accum_out=ss[pb][:, 0:1])
                    _rstd(P, C, ss[pb][:], ms[pb][:], rstd[pb][:], 1, (ss_b[pb],), (ms_b[pb],), (rstd_b[pb],))
                    P.stt(tmp[pb][:], pm[pb][:], rstd[pb][:, 0:1], gbc[:], ALU.mult, ALU.mult,
                          (pm_b[pb], rstd_b[pb], gbc_b), (tmp_b[pb],))
                    P.tt("gpsimd", xt[b][:, s, :], xt[b][:, s, :], tmp[pb][:], ALU.add,
                         (xt_b[b][s], tmp_b[pb]), (xt_b[b][s],))
                P.dma("sync", C.out[i * NS:(i + 1) * NS, :].rearrange("(s p) d -> p s d", p=128), xt[b][:],
                      d_st[b], (xt_b[b][0], xt_b[b][1]), (C.xres_b,))

            load(0)
            load(1)
            wi_ = 0
            for i in range(NT):
                tile(i)
                if i + 2 < NT:
                    load(i + 2)
                for _ in range(4):
                    if wi_ < len(witems):
                        dst, src, dbuf, n = witems[wi_]
                        wi_ += 1
                        wload(dst, src, dbuf, n)
            assert wi_ == len(witems)
            P.barrier()
            P.flush()

        with ExitStack() as es:
            sb, ps = _scope(nc, es, "p4bl%d" % l)
            xt = [sb("xt%d" % i, [128, 2, D], F32) for i in range(2)]
            hb = [sb("hb%d" % i, [128, 2, D], BF16) for i in range(2)]
            hT = [sb("hT%d" % i, [128, 8, NS], BF16) for i in range(2)]
            rf = [sb("rf%d" % i, [128, 2, NS], F32) for i in range(2)]
            hid = sb("hid", [128, 32, NS], BF16)
            tmp = [sb("tmp%d" % i, [128, D], F32) for i in range(2)]
            sq = sb("sq", [128, D], BF16)
            ss = [sb("ss%d" % i, [128, 2], F32) for i in range(2)]
            ms = [sb("ms%d" % i, [128, 2], F32) for i in range(2)]
            rstd = [sb("rstd%d" % i, [128, 2], F32) for i in range(2)]
            ss2 = [sb("ss2%d" % i, [128, 1], F32) for i in range(2)]
            ms2 = [sb("ms2%d" % i, [128, 1], F32) for i in range(2)]
            rstd2 = [sb("rstd2%d" % i, [128, 1], F32) for i in range(2)]
            tp = [ps("tp%d" % i, [128, 8, 128], BF16) for i in range(2)]
            pf = [ps("pf%d" % i, [128, 2, NS], F32) for i in range(2)]
            po = [ps("po%d" % i, [128, D], F32) for i in range(2)]
            xt_b = [[Buf("xt") for _ in range(2)] for _ in range(2)]
            hb_b = [Buf("hb") for _ in range(2)]
            hT_b = [Buf("hT") for _ in range(2)]
            rf_b = [Buf("rf") for _ in range(2)]
            hid_b = [Buf("hid") for _ in range(32)]
            tmp_b = [Buf("tmp") for _ in range(2)]
            ss_b = [Buf("ss") for _ in range(2)]
            ms_b = [Buf("ms") for _ in range(2)]
            rstd_b = [Buf("rstd") for _ in range(2)]
            ss2_b = [Buf("ss2") for _ in range(2)]
            ms2_b = [Buf("ms2") for _ in range(2)]
            rstd2_b = [Buf("rstd2") for _ in range(2)]
            tp_b = [Buf("tp") for _ in range(2)]
            pf_b = [Buf("pf") for _ in range(2)]
            po_b = [Buf("po") for _ in range(2)]
            d_x = [P.dsem("x") for _ in range(2)]
            d_st = [P.dsem("st") for _ in range(2)]
            tpc = [0]
            fc = [0]
            oc = [0]

            def load(i):
                b = i % 2
                P.dma("sync", xt[b][:], C.out[i * NS:(i + 1) * NS, :].rearrange("(s p) d -> p s d", p=128),
                      d_x[b], (C.xres_b,), (xt_b[b][0], xt_b[b][1]))

            def norm(i):
                b = i % 2
                for s in range(2):
                    P.act(sq[:], xt[b][:, s, :], AF.Square, (xt_b[b][s],), (ss_b[b],), accum_out=ss[b][:, s:s + 1])
                _rstd(P, C, ss[b][:], ms[b][:], rstd[b][:], 2, (ss_b[b],), (ms_b[b],), (rstd_b[b],))
                for s in range(2):
                    P.stt(hb[b][:, s, :], xt[b][:, s, :], rstd[b][:, s:s + 1], g2bc[:], ALU.mult, ALU.mult,
                          (xt_b[b][s], rstd_b[b], g2_b), (hb_b[b],))

            def transp(i):
                b = i % 2
                for s in range(2):
                    pb = tpc[0] % 2
                    tpc[0] += 1
                    for k in range(8):
                        P.tr(tp[pb][:, k, :], hb[b][:, s, k * 128:(k + 1) * 128], C.ident[:],
                             (hb_b[b], C.ident_b), (tp_b[pb],), sig=(k == 7))
                    P.acopy(hT[b][:, :, s * 128:(s + 1) * 128], tp[pb][:], (tp_b[pb],), (hT_b[b],))

            def ffn(i):
                b = i % 2
                for c2 in range(16):
                    fb = fc[0] % 2
                    fc[0] += 1
                    for cc in range(2):
                        c = c2 * 2 + cc
                        for k in range(8):
                            P.mm(pf[fb][:, cc, :], w1[:, k, c * 128:(c + 1) * 128], hT[b][:, k, :], k == 0, k == 7,
                                 (hT_b[b], w1_b[k]), (pf_b[fb],), sig=(k == 7 and cc == 1))
                    P.act(rf[fb][:], pf[fb][:], AF.Relu, (pf_b[fb],), (rf_b[fb],))
                    P.tt("vector", hid[:, c2 * 2:c2 * 2 + 2, :], rf[fb][:], rf[fb][:], ALU.mult,
                         (rf_b[fb],), (hid_b[c2 * 2], hid_b[c2 * 2 + 1]))
                for s in range(2):
                    ob = oc[0] % 2
                    oc[0] += 1
                    for nt in range(2):
                        for kc in range(32):
                            P.mm(po[ob][:, nt * 512:(nt + 1) * 512], hid[:, kc, s * 128:(s + 1) * 128],
                                 w2[:, kc, nt * 512:(nt + 1) * 512], kc == 0, kc == 31,
                                 (hid_b[kc], w2_b[kc]), (po_b[ob],), sig=(kc == 31 and nt == 1))
                    P.act(sq[:], po[ob][:], AF.Square, (po_b[ob],), (ss2_b[ob],), accum_out=ss2[ob][:, 0:1])
                    _rstd(P, C, ss2[ob][:], ms2[ob][:], rstd2[ob][:], 1, (ss2_b[ob],), (ms2_b[ob],), (rstd2_b[ob],))
                    P.stt(tmp[ob][:], po[ob][:], rstd2[ob][:, 0:1], g3bc[:], ALU.mult, ALU.mult,
                          (po_b[ob], rstd2_b[ob], g3_b), (tmp_b[ob],))
                    P.tt("gpsimd", xt[b][:, s, :], xt[b][:, s, :], tmp[ob][:], ALU.add,
                         (xt_b[b][s], tmp_b[ob]), (xt_b[b][s],))
                P.dma("sync", C.out[i * NS:(i + 1) * NS, :].rearrange("(s p) d -> p s d", p=128), xt[b][:],
                      d_st[b], (xt_b[b][0], xt_b[b][1]), (C.xres_b,))

            load(0)
            load(1)
            norm(0)
            transp(0)
            for i in range(NT):
                if i + 1 < NT:
                    norm(i + 1)
                    transp(i + 1)
                ffn(i)
                if i + 2 < NT:
                    load(i + 2)
            P.barrier()
            P.flush()


def build(n_layers=DEPTH, debug=False, upto=4, wd=DEPTH):
    nc = bass.Bass("TRN2", target_bir_lowering=False)
    P = Prog(nc)
    C = Ctx()

    def din(name, shape, dtype=F32):
        return nc.dram_tensor(name, list(shape), dtype, kind="ExternalInput").ap()

    C.x = din("x", [T, D])
    C.mix_norm_pre = din("mix_norm_pre", [wd, D])
    C.mix_norm_post = din("mix_norm_post", [wd, D])
    C.mlp_norm_pre = din("mlp_norm_pre", [wd, D])
    C.mlp_norm_post = din("mlp_norm_post", [wd, D])
    C.w_in = din("w_in", [wd, D, 2560])
    C.w_rgate = din("w_rgate", [wd, 8, 64, 64])
    C.w_igate = din("w_igate", [wd, 8, 64, 64])
    C.w_out = din("w_out", [wd, D, D])
    C.w_ff_in = din("w_ff_in", [wd, D, 4096])
    C.w_ff_out = din("w_ff_out", [wd, 4096, D])
    C.lvec = din("lvec", [128, wd * 32])
    C.ident_d = din("ident", [128, 128], BF16)
    C.maskb_d = din("maskb", [128, 256], BF16)
    C.out = nc.dram_tensor("out", [T, D], F32, kind="ExternalOutput").ap()
    sk = "ExternalOutput" if debug else "Internal"
    C.zqkv = nc.dram_tensor("zqkv", [1536, T], BF16, kind=sk).ap()
    C.zlru = nc.dram_tensor("zlru", [1024, T], F32, kind=sk).ap()
    C.mixT = nc.dram_tensor("mixT", [1024, T], BF16, kind=sk).ap()
    C.xres_b, C.zqkv_b, C.zlru_b, C.mixT_b = Buf("xres"), Buf("zqkv"), Buf("zlru"), Buf("mixT")

    with P.es:
        es = P.es
        sb, _ = _scope(nc, es, "glob")
        C.ident = sb("ident", [128, 128], BF16)
        C.maskb = sb("maskb", [128, 256], BF16)
        C.lv = sb("lv", [128, wd * 32], F32)
        C.lvn = sb("lvn", [128, wd * 32], F32)
        C.lc1 = sb("lc1", [128, wd * 32], F32)
        C.lc2 = sb("lc2", [128, wd * 32], F32)
        ltmp = sb("ltmp", [128, wd * 32], F32)
        ltmp2 = sb("ltmp2", [128, wd * 32], F32)
        C.neghalf = sb("neghalf", [128, 4], F32)
        C.one = sb("one", [128, 1], F32)
        C.ident_b, C.maskb_b, C.lv_b, C.lvn_b = Buf("ident"), Buf("maskb"), Buf("lv"), Buf("lvn")
        C.lc1_b, C.lc2_b = Buf("lc1"), Buf("lc2")
        lt_b, lt2_b, nh_b, one_b = Buf("lt"), Buf("lt2"), Buf("nh"), Buf("one")
        d0 = [P.dsem("c%d" % i) for i in range(3)]
        P.dma("sync", C.ident[:], C.ident_d, d0[0], (), (C.ident_b,))
        P.dma("sync", C.maskb[:], C.maskb_d, d0[1], (), (C.maskb_b,))
        P.dma("sync", C.lv[:], C.lvec, d0[2], (), (C.lv_b,))
        P.op("gpsimd", lambda e: e.memset(C.neghalf[:], -0.5), (), (nh_b,))
        P.op("gpsimd", lambda e: e.memset(C.one[:], 1.0), (), (one_b,))
        P.ts("vector", C.lvn[:], C.lv[:], -1.0, None, ALU.mult, None, (C.lv_b,), (C.lvn_b,))
        P.act(ltmp[:], C.lv[:], AF.Exp, (C.lv_b,), (lt_b,), scale=-1.0)
        P.act(ltmp2[:], ltmp[:], AF.Ln, (lt_b, one_b), (lt2_b,), bias=C.one[:, 0:1])
        P.ts("vector", C.lc1[:], ltmp2[:], -8.0, None, ALU.mult, None, (lt2_b,), (C.lc1_b,))
        P.ts("vector", C.lc2[:], ltmp2[:], -16.0, None, ALU.mult, None, (lt2_b,), (C.lc2_b,))
        P.barrier()
        P.flush()

        for l in range(n_layers):
            xsrc = C.x if l == 0 else C.out
            phases = DBG.get("phases", (1, 2, 3, 4))
            if upto >= 1 and 1 in phases:
                P.set_tag("p1")
                phase1(P, C, l, xsrc)
            if upto >= 2 and 2 in phases:
                P.set_tag("p2")
                phase2(P, C, l)
            if upto >= 3 and 3 in phases:
                P.set_tag("p3")
                phase3(P, C, l)
            if upto >= 4 and 4 in phases:
                P.set_tag("p4")
                phase4(P, C, l, xsrc)
    return nc


def host_consts():
    ident = np.eye(128, dtype=np.float32).astype(ml_dtypes.bfloat16)
    k = np.arange(128)[:, None]
    q = np.arange(256)[None, :]
    valid = np.where(q < 128, k <= q, k >= q - 128)
    maskb = np.where(valid, 0.0, -30000.0).astype(np.float32).astype(ml_dtypes.bfloat16)
    return ident, maskb


def make_lvec(conv_w, conv_b, b_rgate, b_igate, lru_lambda):
    lv = np.zeros((128, DEPTH * 32), np.float32)
    for l in range(DEPTH):
        for j in range(4):
            sl = slice(j * 128, (j + 1) * 128)
            o = l * 32 + j * 8
            for k in range(4):
                lv[:, o + k] = conv_w[l, k, sl]
            lv[:, o + 4] = conv_b[l, sl]
            lv[:, o + 5] = b_rgate[l, sl]
            lv[:, o + 6] = b_igate[l, sl]
            lv[:, o + 7] = lru_lambda[l, sl]
    return lv


def make_in_maps(inputs, ncores=NCORES):
    ident, maskb = host_consts()
    f = lambda a: np.ascontiguousarray(np.asarray(a, dtype=np.float32))
    lvec = make_lvec(f(inputs["conv_w"]), f(inputs["conv_b"]), f(inputs["b_rgate"]), f(inputs["b_igate"]),
                     f(inputs["lru_lambda"]))
    shared = {k: f(inputs[k]) for k in ("mix_norm_pre", "mix_norm_post", "mlp_norm_pre", "mlp_norm_post", "w_in",
                                        "w_rgate", "w_igate", "w_out", "w_ff_in", "w_ff_out")}
    shared.update(lvec=lvec, ident=ident, maskb=maskb)
    x = f(inputs["x"])
    return [dict(shared, x=np.ascontiguousarray(x[c])) for c in range(ncores)]


def make_in_maps_layer(inputs, l, x, ncores=NCORES):
    ident, maskb = host_consts()
    f = lambda a: np.ascontiguousarray(np.asarray(a, dtype=np.float32))
    lvec = make_lvec(f(inputs["conv_w"]), f(inputs["conv_b"]), f(inputs["b_rgate"]), f(inputs["b_igate"]),
                     f(inputs["lru_lambda"]))
    shared = {k: f(np.asarray(inputs[k])[l:l + 1]) for k in
              ("mix_norm_pre", "mix_norm_post", "mlp_norm_pre", "mlp_norm_post", "w_in",
               "w_rgate", "w_igate", "w_out", "w_ff_in", "w_ff_out")}
    shared.update(lvec=np.ascontiguousarray(lvec[:, l * 32:(l + 1) * 32]), ident=ident, maskb=maskb)
    return [dict(shared, x=np.ascontiguousarray(x[c])) for c in range(ncores)]


def kernel(**inputs):
    x = np.ascontiguousarray(np.asarray(inputs["x"], dtype=np.float32))
    for l in range(DEPTH):
        nc = build(n_layers=1, wd=1)
        in_maps = make_in_maps_layer(inputs, l, x)
        res = run_bass_kernel_spmd(nc, in_maps, core_ids=list(range(NCORES)))
        x = np.stack([np.asarray(r["out"], dtype=np.float32) for r in res.results], axis=0)
    return x
```
